# Optimizing a Trainium2 kernel written in Bass

```python
import jax, jax.numpy as jnp
from jax import lax
import numpy as np

D_MODEL = 1024
BATCH = 1
SEQ = 16384
DEPTH = 2

HEAD_DIM = 64
ATTN_HEADS = 8
ATTN_WIDTH = ATTN_HEADS * HEAD_DIM
GLA_HEADS = 4
GLA_DK = 64
GLA_DV = 128
GLA_K_WIDTH = GLA_HEADS * GLA_DK
GLA_V_WIDTH = GLA_HEADS * GLA_DV
GLA_GATE_RANK = 16
GLA_TAU = 16.0
GLA_CHUNK = 64
MIX_WIDTH = ATTN_WIDTH + GLA_V_WIDTH
IN_WIDTH = 3 * ATTN_WIDTH + 2 * GLA_K_WIDTH + 2 * GLA_V_WIDTH + GLA_GATE_RANK
D_FF = -(-8 * D_MODEL // (3 * 256)) * 256
ROPE_THETA = 500000.0
ROPE_DIM = HEAD_DIM // 4
DILATED_PATTERNS = ((128, 1), (512, 4), (2048, 16))
N_PATTERNS = len(DILATED_PATTERNS)
N_KEYS = DILATED_PATTERNS[0][0] // DILATED_PATTERNS[0][1] + 1
Q_BLOCK = 128
RMS_EPS = 1e-6

kernel_name = "hymba_dilated_gla_hybrid"

F32 = jnp.float32


def rmsnorm(x, g):
    xf = x.astype(F32)
    y = xf * lax.rsqrt(jnp.mean(xf * xf, axis=-1, keepdims=True) + RMS_EPS)
    return (y * g.astype(F32)).astype(x.dtype)


def rope_partial(x, pos):
    half = ROPE_DIM // 2
    inv = ROPE_THETA ** (-(jnp.arange(half, dtype=F32) * 2.0) / ROPE_DIM)
    ang = pos.astype(F32)[:, None] * inv[None, :]
    cos = jnp.cos(ang)[None, :, None, :]
    sin = jnp.sin(ang)[None, :, None, :]
    xr = x[..., :ROPE_DIM].astype(F32)
    x1, x2 = xr[..., :half], xr[..., half:]
    rot = jnp.concatenate([x1 * cos - x2 * sin, x2 * cos + x1 * sin], axis=-1).astype(x.dtype)
    return jnp.concatenate([rot, x[..., ROPE_DIM:]], axis=-1)


def dilated_attention(q, k, v):
    B, S, H, hd = q.shape
    nblk = S // Q_BLOCK
    dists = jnp.concatenate([d * jnp.arange(w // d + 1) for (w, d) in DILATED_PATTERNS])
    q_blocks = q.reshape(B, nblk, Q_BLOCK, H, hd).transpose(1, 0, 2, 3, 4)
    starts = jnp.arange(nblk) * Q_BLOCK
    scale = hd ** -0.5

    def block(args):
        q_blk, start = args
        pos = start + jnp.arange(Q_BLOCK)
        idx = pos[:, None] - dists[None, :]
        valid = idx >= 0
        idx_c = jnp.maximum(idx, 0)
        kg = jnp.take(k, idx_c, axis=1).astype(F32)
        vg = jnp.take(v, idx_c, axis=1).astype(F32)
        s = jnp.einsum('bqhd,bqnhd->bhqn', q_blk.astype(F32), kg) * scale
        s = jnp.where(valid[None, None], s, -jnp.inf)
        s = s.reshape(B, H, Q_BLOCK, N_PATTERNS, N_KEYS)
        lse = jax.nn.logsumexp(s, axis=-1, keepdims=True)
        p = jnp.exp(s - lse)
        o_pat = jnp.einsum('bhqpn,bqpnhd->bhqpd', p,
                           vg.reshape(B, Q_BLOCK, N_PATTERNS, N_KEYS, H, hd))
        w = jax.nn.softmax(lse, axis=-2)
        o = jnp.sum(w * o_pat, axis=-2)
        return o.transpose(0, 2, 1, 3).astype(q.dtype)

    out = lax.map(block, (q_blocks, starts))
    return out.transpose(1, 0, 2, 3, 4).reshape(B, S, H, hd)


def gla_chunked(q, k, v, g):
    B, S, H, dk = q.shape
    dv = v.shape[-1]
    C = GLA_CHUNK
    nc = S // C

    def chunks(t):
        return t.astype(F32).reshape(B, nc, C, H, t.shape[-1]).transpose(1, 0, 3, 2, 4)

    causal = jnp.tril(jnp.ones((C, C), dtype=bool))

    def step(state, inp):
        qc, kc, vc, gc = inp
        b = jnp.cumsum(gc, axis=-2)
        o_inter = jnp.einsum('bhtk,bhkv->bhtv', qc * jnp.exp(b), state)
        diff = b[:, :, :, None, :] - b[:, :, None, :, :]
        decay = jnp.exp(jnp.where(causal[None, None, :, :, None], diff, -jnp.inf))
        attn = jnp.einsum('bhtk,bhsk,bhtsk->bhts', qc, kc, decay)
        o_intra = jnp.einsum('bhts,bhsv->bhtv', attn, vc)
        b_last = b[:, :, -1:, :]
        state = jnp.exp(b_last[:, :, 0, :])[..., None] * state + \
            jnp.einsum('bhsk,bhsv->bhkv', kc * jnp.exp(b_last - b), vc)
        return state, o_intra + o_inter

    s0 = jnp.zeros((B, H, dk, dv), F32)
    _, o = lax.scan(step, s0, (chunks(q) * (dk ** -0.5), chunks(k), chunks(v), chunks(g)))
    return o.transpose(1, 0, 3, 2, 4).reshape(B, S, H, dv).astype(v.dtype)


def hybrid_mixer(h, w_in, gla_w_gate_up, gla_b_gate, gla_out_norm, attn_out_norm, w_out):
    B, S, _ = h.shape
    proj = h @ w_in
    cuts = [ATTN_WIDTH, 2 * ATTN_WIDTH, 3 * ATTN_WIDTH,
            3 * ATTN_WIDTH + GLA_K_WIDTH, 3 * ATTN_WIDTH + 2 * GLA_K_WIDTH,
            3 * ATTN_WIDTH + 2 * GLA_K_WIDTH + GLA_V_WIDTH,
            3 * ATTN_WIDTH + 2 * GLA_K_WIDTH + 2 * GLA_V_WIDTH]
    aq, ak, av, gq, gk, gv, gr, g_low = jnp.split(proj, cuts, axis=-1)

    pos = jnp.arange(S)
    aq = rope_partial(aq.reshape(B, S, ATTN_HEADS, HEAD_DIM), pos)
    ak = rope_partial(ak.reshape(B, S, ATTN_HEADS, HEAD_DIM), pos)
    av = av.reshape(B, S, ATTN_HEADS, HEAD_DIM)
    a_out = dilated_attention(aq, ak, av).reshape(B, S, ATTN_WIDTH)
    a_out = rmsnorm(a_out, attn_out_norm)

    z = g_low @ gla_w_gate_up + gla_b_gate
    log_alpha = jax.nn.log_sigmoid(z.astype(F32)) / GLA_TAU
    o = gla_chunked(gq.reshape(B, S, GLA_HEADS, GLA_DK),
                    gk.reshape(B, S, GLA_HEADS, GLA_DK),
                    gv.reshape(B, S, GLA_HEADS, GLA_DV),
                    log_alpha.reshape(B, S, GLA_HEADS, GLA_DK))
    o = rmsnorm(o, gla_out_norm.reshape(GLA_HEADS, GLA_DV)).reshape(B, S, GLA_V_WIDTH)
    g_out = o * jax.nn.silu(gr)

    mixed = jnp.concatenate([a_out, g_out], axis=-1)
    return mixed @ w_out


def swiglu(h, w_gate, w_up, w_down):
    return (jax.nn.silu(h @ w_gate) * (h @ w_up)) @ w_down


def setup_inputs(seed: int = 0) -> dict:
    key = jax.random.key(seed)
    ks = jax.random.split(key, 16)
    L, D = DEPTH, D_MODEL

    def nrm(k, shape, fan_in):
        return jax.random.normal(k, shape, F32) * (fan_in ** -0.5)

    def gain(k, shape):
        return 1.0 + 0.05 * jax.random.normal(k, shape, F32)

    return {
        "x": jax.random.normal(ks[0], (BATCH, SEQ, D), F32),
        "mix_pre_norm": gain(ks[1], (L, D)),
        "mix_post_norm": gain(ks[2], (L, D)),
        "ffn_pre_norm": gain(ks[3], (L, D)),
        "ffn_post_norm": gain(ks[4], (L, D)),
        "w_in": nrm(ks[5], (L, D, IN_WIDTH), D),
        "gla_w_gate_up": nrm(ks[6], (L, GLA_GATE_RANK, GLA_K_WIDTH), GLA_GATE_RANK),
        "gla_b_gate": 0.1 * jax.random.normal(ks[7], (L, GLA_K_WIDTH), F32),
        "gla_out_norm": gain(ks[8], (L, GLA_V_WIDTH)),
        "attn_out_norm": gain(ks[9], (L, ATTN_WIDTH)),
        "w_out": nrm(ks[10], (L, MIX_WIDTH, D), MIX_WIDTH),
        "w_gate": nrm(ks[11], (L, D, D_FF), D),
        "w_up": nrm(ks[12], (L, D, D_FF), D),
        "w_down": nrm(ks[13], (L, D_FF, D), D_FF),
    }


def reference(x, mix_pre_norm, mix_post_norm, ffn_pre_norm, ffn_post_norm, w_in,
              gla_w_gate_up, gla_b_gate, gla_out_norm, attn_out_norm, w_out,
              w_gate, w_up, w_down):
    h = x
    for l in range(DEPTH):
        m = hybrid_mixer(rmsnorm(h, mix_pre_norm[l]), w_in[l], gla_w_gate_up[l], gla_b_gate[l],
                         gla_out_norm[l], attn_out_norm[l], w_out[l])
        h = h + rmsnorm(m, mix_post_norm[l])
        f = swiglu(rmsnorm(h, ffn_pre_norm[l]), w_gate[l], w_up[l], w_down[l])
        h = h + rmsnorm(f, ffn_post_norm[l])
    return h
```

```python
from contextlib import ExitStack
import numpy as np
import ml_dtypes
import concourse.bass as bass
import concourse.mybir as mybir
from concourse.bass_utils import run_bass_kernel_spmd

F32 = mybir.dt.float32
BF16 = mybir.dt.bfloat16
AF = mybir.ActivationFunctionType
ALU = mybir.AluOpType

NCORES = 8
TPC = 2048
NT = TPC // 128
NG = TPC // 512
D = 1024
KC = 8
INW = 3088
DFF = 2816
FC = DFF // 128
EPS = 1e-6
MASKW = 2944
LN8 = float(np.log(0.125))
DBG = {}


class Buf:
    __slots__ = ("name", "w", "r", "excl")

    def __init__(self, name="", excl=False):
        self.name = name
        self.w = None
        self.r = []
        self.excl = excl


class Ev:
    __slots__ = ("eng", "sem", "val", "key")

    def __init__(self, eng, sem, val, key):
        self.eng = eng
        self.sem = sem
        self.val = val
        self.key = key


class Sched:
    NDMA = 8

    def __init__(self, nc, same_engine_sync=True):
        self.nc = nc
        self.E = {"pe": nc.tensor, "act": nc.scalar, "dve": nc.vector, "pool": nc.gpsimd, "sp": nc.sync}
        self.sem = {e: nc.alloc_semaphore("c_" + e) for e in self.E}
        self.cnt = {e: 0 for e in self.E}
        self.pending = {e: [] for e in self.E}
        self.seen = {e: {} for e in self.E}
        self.dsem = {e: [nc.alloc_semaphore("d_%s%d" % (e, i)) for i in range(self.NDMA)] for e in ("sp", "pool")}
        self.dcnt = {e: 0 for e in self.dsem}
        self.ses = same_engine_sync
        self.all_events = {}

    def _need(self, e, reads, writes):
        evs = {}
        for b in reads:
            if b.w is not None:
                evs[id(b.w)] = b.w
        for b in writes:
            if b.w is not None:
                evs[id(b.w)] = b.w
            for r in b.r:
                evs[id(r)] = r
        out = {}
        for ev in evs.values():
            if ev.eng == e and ev.key == "c_" + e and (e == "pe" or not self.ses):
                continue
            if ev.val is None:
                raise RuntimeError("wait on un-inc'd op of %s" % ev.eng)
            if self.seen[e].get(ev.key, 0) >= ev.val:
                continue
            if ev.key not in out or out[ev.key].val < ev.val:
                out[ev.key] = ev
        return out.values()

    def _wait(self, e, reads, writes):
        for ev in self._need(e, reads, writes):
            self.E[e].wait_ge(ev.sem, ev.val)
            self.seen[e][ev.key] = ev.val

    def _record(self, ev, reads, writes):
        for b in reads:
            b.r = [r for r in b.r if not (r.key == ev.key and r.key.startswith("c_"))]
            b.r.append(ev)
        for b in writes:
            b.w = ev
            b.r = []

    def op(self, e, fn, reads=(), writes=(), inc=True, self_sync=False):
        writes = list(writes) + [b for b in reads if b.excl]
        reads = [b for b in reads if not b.excl]
        self._wait(e, reads, writes)
        if self_sync and self.cnt[e] > 0:
            if self.pending[e]:
                raise RuntimeError("self_sync with pending ops")
            self.E[e].wait_ge(self.sem[e], self.cnt[e])
        ins = fn(self.E[e])
        if inc:
            self.cnt[e] += 1
            ins.then_inc(self.sem[e], 1)
            ev = Ev(e, self.sem[e], self.cnt[e], "c_" + e)
            for p in self.pending[e]:
                p.val = self.cnt[e]
            self.pending[e] = []
        else:
            ev = Ev(e, self.sem[e], None, "c_" + e)
            self.pending[e].append(ev)
        self._record(ev, reads, writes)
        return ev

    def dma(self, e, out, in_, reads=(), writes=(), **kw):
        self._wait(e, reads, writes)
        i = self.dcnt[e]
        self.dcnt[e] += 1
        k = i % self.NDMA
        sem = self.dsem[e][k]
        val = 16 * (i // self.NDMA + 1)
        self.E[e].dma_start(out=out, in_=in_, **kw).then_inc(sem, 16)
        ev = Ev(e, sem, val, "d_%s%d" % (e, k))
        self._record(ev, reads, writes)
        self.all_events[ev.key] = ev
        return ev

    def _all(self):
        evs = list(self.all_events.values())
        for x in self.E:
            if self.cnt[x] > 0:
                if self.pending[x]:
                    raise RuntimeError("pending un-inc'd ops on " + x)
                evs.append(Ev(x, self.sem[x], self.cnt[x], "c_" + x))
        return evs

    def barrier(self):
        evs = self._all()
        for e in self.E:
            for ev in evs:
                if ev.eng == e and ev.key == "c_" + e:
                    continue
                if self.seen[e].get(ev.key, 0) >= ev.val:
                    continue
                self.E[e].wait_ge(ev.sem, ev.val)
                self.seen[e][ev.key] = ev.val

    def finish(self, e="sp"):
        for ev in self._all():
            if ev.eng == e and ev.key == "c_" + e:
                continue
            if self.seen[e].get(ev.key, 0) >= ev.val:
                continue
            self.E[e].wait_ge(ev.sem, ev.val)
            self.seen[e][ev.key] = ev.val


class Rot:
    def __init__(self, items):
        self.items = items
        self.i = 0

    def next(self):
        it = self.items[self.i % len(self.items)]
        self.i += 1
        return it


class Ctx:
    def __init__(self, name):
        self.nc = bass.Bass("TRN2", target_bir_lowering=False)
        self.S = Sched(self.nc)
        self.uid = 0
        self.dr = {}
        self.name = name

    def din(self, name, shape, dt=F32):
        ap = self.nc.dram_tensor(name, list(shape), dt, kind="ExternalInput").ap()
        self.dr[name] = ap
        return ap

    def dout(self, name, shape, dt=F32):
        ap = self.nc.dram_tensor(name, list(shape), dt, kind="ExternalOutput").ap()
        self.dr[name] = ap
        return ap

    def sb(self, st, shape, dt=F32, name="t"):
        self.uid += 1
        nm = "%s_%d" % (name, self.uid)
        t = st.enter_context(self.nc.sbuf_tensor(nm, list(shape), dt))
        return t, Buf(nm)

    def ps(self, st, shape, dt=F32, name="ps"):
        self.uid += 1
        nm = "%s_%d" % (name, self.uid)
        t = st.enter_context(self.nc.psum_tensor(nm, list(shape), dt))
        return t, Buf(nm, excl=True)


def setup_common(c, st, nslab=3):
    S = c.S
    c.banks = Rot([c.ps(st, [128, 512], F32, "bk") for _ in range(6)])
    c.tbanks = Rot([c.ps(st, [128, 1024], BF16, "tb") for _ in range(2)])
    c.ones_bf, c.ones_bf_b = c.sb(st, [128, 128], BF16, "ones")
    S.op("dve", lambda e: e.memset(c.ones_bf[:], 1.0), writes=[c.ones_bf_b])
    c.ident, c.ident_b = c.sb(st, [128, 128], BF16, "ident")
    S.dma("pool", c.ident[:], c.dr["ident"], writes=[c.ident_b])
    c.slabs = Rot([c.sb(st, [128, KC, 512], BF16, "slab") for _ in range(nslab)]) if nslab else None
    c.sq = Rot([c.sb(st, [128, 512], BF16, "sq") for _ in range(3)])
    c.rs = Rot([c.sb(st, [128, 512], F32, "rs") for _ in range(2)])
    c.tmpf = Rot([c.sb(st, [128, 512], F32, "tmpf") for _ in range(3)])


def load_slab(c, wd, col0, ncols):
    t, b = c.slabs.next()
    src = wd[:, col0:col0 + ncols].rearrange("(k p) n -> p k n", p=128)
    c.S.dma("pool", t[:, :, 0:ncols], src, writes=[b])
    return t, b


def rmsnorm_fm(c, src, srcb, nch, N, gain, gainb, g, dst_fn, dstb, add_into=None):
    S = c.S
    pst, psb = c.banks.next()
    for ci in range(nch):
        sq, sqb = c.sq.next()
        S.op("act", lambda e, ci=ci, sq=sq: e.activation(out=sq[:], in_=src(ci), func=AF.Square),
             reads=srcb, writes=[sqb])
        S.op("pe", lambda e, ci=ci, sq=sq: e.matmul(pst[:], lhsT=c.ones_bf[:], rhs=sq[:], start=(ci == 0),
                                                    stop=(ci == nch - 1)),
             reads=[sqb, c.ones_bf_b], writes=[psb], inc=True)
    rs, rsb = c.rs.next()
    S.op("act", lambda e: e.activation(out=rs[:], in_=pst[:], func=AF.Ln, scale=1.0 / N, bias=EPS),
         reads=[psb], writes=[rsb])
    S.op("act", lambda e: e.activation(out=rs[:], in_=rs[:], func=AF.Exp, scale=-0.5), reads=[rsb], writes=[rsb])
    for ci in range(nch):
        if add_into is None:
            S.op("dve", lambda e, ci=ci: e.scalar_tensor_tensor(out=dst_fn(ci), in0=src(ci), scalar=gain[:, ci:ci + 1],
                                                                 in1=rs[:], op0=ALU.mult, op1=ALU.mult),
                 reads=srcb + [gainb, rsb], writes=dstb)
        else:
            tf, tfb = c.tmpf.next()
            S.op("dve", lambda e, ci=ci, tf=tf: e.scalar_tensor_tensor(out=tf[:], in0=src(ci), scalar=gain[:, ci:ci + 1],
                                                                        in1=rs[:], op0=ALU.mult, op1=ALU.mult),
                 reads=srcb + [gainb, rsb], writes=[tfb])
            S.op("dve", lambda e, ci=ci, tf=tf: e.tensor_tensor(out=add_into(ci), in0=add_into(ci), in1=tf[:], op=ALU.add),
                 reads=[tfb] + dstb, writes=dstb)


def proj_fm(c, slab, slabb, cc, hn, hnb, g, kc=KC):
    pst, psb = c.banks.next()
    for k in range(kc):
        c.S.op("pe", lambda e, k=k: e.matmul(pst[:], lhsT=slab[:, k, cc * 128:(cc + 1) * 128],
                                             rhs=hn[:, k, g * 512:(g + 1) * 512], start=(k == 0), stop=(k == kc - 1)),
               reads=[slabb, hnb], writes=[psb], inc=(k == kc - 1))
    return pst, psb


def proj_tm(c, slab, slabb, ncols, hn, hnb, t):
    pst, psb = c.banks.next()
    for k in range(KC):
        c.S.op("pe", lambda e, k=k: e.matmul(pst[:, 0:ncols], lhsT=hn[:, k, t * 128:(t + 1) * 128],
                                             rhs=slab[:, k, 0:ncols], start=(k == 0), stop=(k == KC - 1)),
               reads=[slabb, hnb], writes=[psb], inc=(k == KC - 1))
    return pst, psb


def load_hT(c, st, src_name):
    hT, hTb = c.sb(st, [128, KC, TPC], F32, "hT")
    src = c.dr[src_name].rearrange("(k p) t -> p k t", p=128)
    for k in range(KC):
        c.S.dma("sp", hT[:, k, :], src[:, k, :], writes=[hTb])
    return hT, hTb


def build_A():
    c = Ctx("A")
    nc, S = c.nc, c.S
    c.din("hT", [D, TPC])
    w_in = c.din("w_in", [D, INW])
    c.din("wgu", [16, 256])
    c.din("bg", [1, 256])
    c.din("g_pre", [128, KC])
    c.din("ident", [128, 128])
    c.din("trirev", [128, 128])
    c.din("rm", [128, 128])
    c.din("ropeC", [128, TPC])
    c.din("ropeS", [128, TPC])
    o_qT = c.dout("qT", [4, 128, TPC], BF16)
    o_kT = c.dout("kT", [4, 128, TPC], BF16)
    o_vA = c.dout("vA", [4, 128, NT, 130], BF16)
    o_gqT = c.dout("gqT", [2, 128, TPC], BF16)
    o_gkT = c.dout("gkT", [2, 128, TPC], BF16)
    o_gv = c.dout("gv", [2, 128, NT, 256], BF16)
    o_gr = c.dout("gr", [2, 128, NT, 256], BF16)
    o_lq = c.dout("lq", [2, 128, NT, 128], F32)
    o_Sloc = c.dout("Sloc", [128, 2, 256], F32)
    o_Dtot = c.dout("Dtot", [2, 128, 1], F32)

    with ExitStack() as st:
        setup_common(c, st)
        hn, hnb = c.sb(st, [128, KC, TPC], BF16, "hn")
        gpre, gpreb = c.sb(st, [128, KC], F32, "gpre")
        S.dma("sp", gpre[:], c.dr["g_pre"], writes=[gpreb])
        with ExitStack() as st2:
            hT, hTb = load_hT(c, st2, "hT")
            for g in range(NG):
                gs = slice(g * 512, (g + 1) * 512)
                rmsnorm_fm(c, lambda ci: hT[:, ci, gs], [hTb], KC, D, gpre, gpreb, g,
                           lambda ci: hn[:, ci, gs], [hnb])
            S.barrier()
        if DBG.get("A_stop") == 1:
            S.finish("sp")
            return c
        rm, rmb = c.sb(st, [128, 128], BF16, "rm")
        S.dma("pool", rm[:], c.dr["rm"], writes=[rmb])
        rC, rCb = c.sb(st, [128, TPC], BF16, "rC")
        rS, rSb = c.sb(st, [128, TPC], BF16, "rS")
        S.dma("pool", rC[:], c.dr["ropeC"], writes=[rCb])
        S.dma("pool", rS[:], c.dr["ropeS"], writes=[rSb])
        xs_rot = Rot([c.sb(st, [128, 512], BF16, "xs") for _ in range(2)])
        stg = Rot([c.sb(st, [128, TPC], BF16, "stg") for _ in range(2)])

        for which, o_d in ((0, o_qT), (1, o_kT)):
            slab, slabb = load_slab(c, w_in, which * 512, 512)
            if DBG.get("A_stop") == 11:
                S.finish("sp")
                return c
            for p in range(4):
                sg, sgb = stg.next()
                for g in range(NG):
                    gs = slice(g * 512, (g + 1) * 512)
                    pst, psb = proj_fm(c, slab, slabb, p, hn, hnb, g)
                    xs, xsb = xs_rot.next()
                    if DBG.get("A_stop") == 15:
                        S.finish("sp")
                        return c
                    S.op("act", lambda e, xs=xs, pst=pst: e.activation(out=xs[:], in_=pst[:], func=AF.Copy),
                         reads=[psb], writes=[xsb])
                    if DBG.get("A_stop") == 16:
                        S.finish("sp")
                        return c
                    ps2, ps2b = c.banks.next()
                    S.op("pe", lambda e, xs=xs, ps2=ps2: e.matmul(ps2[:], lhsT=rm[:], rhs=xs[:], start=True, stop=True),
                         reads=[rmb, xsb], writes=[ps2b])
                    t1, t1b = c.tmpf.next()
                    t2, t2b = c.tmpf.next()
                    if DBG.get("A_stop") == 17:
                        S.finish("sp")
                        return c
                    S.op("dve", lambda e, t1=t1, pst=pst, gs=gs: e.tensor_tensor(out=t1[:], in0=pst[:], in1=rC[:, gs], op=ALU.mult),
                         reads=[psb, rCb], writes=[t1b])
                    if DBG.get("A_stop") == 18:
                        S.finish("sp")
                        return c
                    S.op("dve", lambda e, t2=t2, ps2=ps2, gs=gs: e.tensor_tensor(out=t2[:], in0=ps2[:], in1=rS[:, gs], op=ALU.mult),
                         reads=[ps2b, rSb], writes=[t2b])
                    S.op("dve", lambda e, t1=t1, t2=t2, sg=sg, gs=gs: e.tensor_tensor(out=sg[:, gs], in0=t1[:], in1=t2[:], op=ALU.add),
                         reads=[t1b, t2b], writes=[sgb])
                    if DBG.get("A_stop") == 14:
                        S.finish("sp")
                        return c
                if DBG.get("A_stop") == 12:
                    S.finish("sp")
                    return c
                S.dma("sp", o_d[p], sg[:], reads=[sgb])
                if DBG.get("A_stop") == 13:
                    S.finish("sp")
                    return c

        if DBG.get("A_stop") == 2:
            S.finish("sp")
            return c
        with ExitStack() as st2:
            vst, vstb = c.sb(st2, [128, 4, NT, 130], BF16, "vst")
            S.op("dve", lambda e: e.memset(vst[:], 1.0), writes=[vstb])
            slab, slabb = load_slab(c, w_in, 1024, 512)
            for t in range(NT):
                pst, psb = proj_tm(c, slab, slabb, 512, hn, hnb, t)
                dst = vst[:, :, t, :].rearrange("p a (h x) -> p a h x", x=65)[:, :, :, 0:64]
                srcv = pst[:].rearrange("p (a h x) -> p a h x", a=4, h=2)
                S.op("act", lambda e, dst=dst, srcv=srcv: e.activation(out=dst, in_=srcv, func=AF.Copy),
                     reads=[psb], writes=[vstb])
            for p in range(4):
                S.dma("sp", o_vA[p], vst[:, p], reads=[vstb])
            S.barrier()

        if DBG.get("A_stop") == 3:
            S.finish("sp")
            return c
        with ExitStack() as st2:
            gkst, gkstb = c.sb(st2, [128, 2, TPC], BF16, "gkst")
            gvst, gvstb = c.sb(st2, [128, 2, NT, 256], BF16, "gvst")
            lall, lallb = c.sb(st2, [128, 2, NT, 128], F32, "lall")
            glT, glTb = c.sb(st2, [16, TPC], BF16, "glT")
            wgu, wgub = c.sb(st2, [16, 256], BF16, "wgu")
            bgt, bgtb = c.sb(st2, [1, 256], BF16, "bgt")
            S.dma("pool", wgu[:], c.dr["wgu"], writes=[wgub])
            S.dma("pool", bgt[:], c.dr["bg"], writes=[bgtb])
            slab, slabb = load_slab(c, w_in, 1536, 512)
            for which, o_d in ((0, o_gqT), (1, o_gkT)):
                for gp in range(2):
                    if which == 0:
                        sg, sgb = stg.next()
                        dstf = lambda gs, sg=sg: sg[:, gs]
                    else:
                        sg, sgb = gkst, gkstb
                        dstf = lambda gs, gp=gp: gkst[:, gp, gs]
                    for g in range(NG):
                        gs = slice(g * 512, (g + 1) * 512)
                        pst, psb = proj_fm(c, slab, slabb, which * 2 + gp, hn, hnb, g)
                        S.op("act", lambda e, pst=pst, gs=gs, dstf=dstf: e.activation(out=dstf(gs), in_=pst[:], func=AF.Copy),
                             reads=[psb], writes=[sgb])
                    if which == 0:
                        S.dma("sp", o_d[gp], sg[:], reads=[sgb])
                    else:
                        S.dma("sp", o_d[gp], gkst[:, gp, :], reads=[gkstb])
            slab, slabb = load_slab(c, w_in, 3072, 16)
            for g in range(NG):
                gs = slice(g * 512, (g + 1) * 512)
                pst, psb = c.banks.next()
                for k in range(KC):
                    S.op("pe", lambda e, k=k, pst=pst, gs=gs: e.matmul(pst[0:16, :], lhsT=slab[:, k, 0:16], rhs=hn[:, k, gs],
                                                                       start=(k == 0), stop=(k == KC - 1)),
                         reads=[slabb, hnb], writes=[psb], inc=(k == KC - 1))
                S.op("act", lambda e, pst=pst, gs=gs: e.activation(out=glT[:, gs], in_=pst[0:16, :], func=AF.Copy),
                     reads=[psb], writes=[glTb])
            with ExitStack() as st3:
                grst, grstb = c.sb(st3, [128, 2, NT, 256], BF16, "grst")
                for which, (dstt, dsttb, o_d) in enumerate(((gvst, gvstb, o_gv), (grst, grstb, o_gr))):
                    slab, slabb = load_slab(c, w_in, 2048 + which * 512, 512)
                    for t in range(NT):
                        pst, psb = proj_tm(c, slab, slabb, 512, hn, hnb, t)
                        S.op("act", lambda e, pst=pst, t=t, dstt=dstt: e.activation(
                            out=dstt[:, :, t, :], in_=pst[:].rearrange("p (a x) -> p a x", a=2), func=AF.Copy),
                             reads=[psb], writes=[dsttb])
                    for gp in range(2):
                        S.dma("sp", o_d[gp], dstt[:, gp], reads=[dsttb])
                S.barrier()
            if DBG.get("A_stop") == 4:
                S.finish("sp")
                return c
            ez_rot = Rot([c.sb(st2, [128, 256], F32, "ez") for _ in range(2)])
            for t in range(NT):
                ts = slice(t * 128, (t + 1) * 128)
                pst, psb = c.banks.next()
                S.op("pe", lambda e, pst=pst, ts=ts: e.matmul(pst[:, 0:256], lhsT=glT[:, ts], rhs=wgu[:], start=True, stop=False),
                     reads=[glTb, wgub], writes=[psb], inc=False)
                S.op("pe", lambda e, pst=pst: e.matmul(pst[:, 0:256], lhsT=c.ones_bf[0:1, :], rhs=bgt[:], start=False, stop=True),
                     reads=[c.ones_bf_b, bgtb], writes=[psb])
                ez, ezb = ez_rot.next()
                S.op("act", lambda e, ez=ez, pst=pst: e.activation(out=ez[:], in_=pst[:, 0:256], func=AF.Exp, scale=-1.0),
                     reads=[psb], writes=[ezb])
                S.op("act", lambda e, ez=ez, t=t: e.activation(out=lall[:, :, t, :], in_=ez[:].rearrange("p (a x) -> p a x", a=2),
                                                                func=AF.Ln, bias=1.0),
                     reads=[ezb], writes=[lallb])
            for gp in range(2):
                S.dma("sp", o_lq[gp], lall[:, gp], reads=[lallb])
            if DBG.get("A_stop") == 5:
                S.finish("sp")
                return c
            trirev, trirevb = c.sb(st2, [128, 128], F32, "trirev")
            S.dma("sp", trirev[:], c.dr["trirev"], writes=[trirevb])
            onesf, onesfb = c.sb(st2, [128, 2], F32, "onesf")
            S.op("dve", lambda e: e.memset(onesf[:], 1.0), writes=[onesfb])
            for gp in range(2):
                Sst, Sstb = c.sb(st2, [128, 256], F32, "Sst")
                Dsum, Dsumb = c.sb(st2, [128, 1], F32, "Dsum")
                S.op("dve", lambda e, Sst=Sst: e.memset(Sst[:], 0.0), writes=[Sstb])
                S.op("dve", lambda e, Dsum=Dsum: e.memset(Dsum[:], 0.0), writes=[Dsumb])
                gla_prepass(c, st2, lambda t, gp=gp: lall[:, gp, t, :], [lallb],
                            lambda t, gp=gp: gkst[:, gp, t * 128:(t + 1) * 128], [gkstb],
                            lambda t, gp=gp: gvst[:, gp, t, :], [gvstb],
                            trirev, trirevb, onesf, onesfb, Sst, Sstb, Dsum, Dsumb)
                S.dma("sp", o_Sloc[:, gp, :], Sst[:], reads=[Sstb])
                S.dma("sp", o_Dtot[gp], Dsum[:], reads=[Dsumb])
            S.barrier()
        S.finish("sp")
    return c


def gla_prepass(c, st, l_fn, lb, gkT_fn, gkb, gv_fn, gvb, trirev, trirevb, onesf, onesfb, Sst, Sstb, Dsum, Dsumb):
    S = c.S
    e3r = Rot([c.sb(st, [128, 128], F32, "e3") for _ in range(2)])
    ktTr = Rot([c.sb(st, [128, 128], BF16, "ktT") for _ in range(2)])
    ktr = Rot([c.sb(st, [128, 128], BF16, "kt") for _ in range(2)])
    decr = Rot([c.sb(st, [128, 2], F32, "dec") for _ in range(2)])
    for t in range(NT):
        pcr, pcrb = c.banks.next()
        S.op("pe", lambda e, t=t, pcr=pcr: e.matmul(pcr[:, 0:128], lhsT=l_fn(t), rhs=trirev[:], start=True, stop=True),
             reads=lb + [trirevb], writes=[pcrb])
        S.op("pe", lambda e, t=t, pcr=pcr: e.matmul(pcr[:, 128:130], lhsT=l_fn(t), rhs=onesf[:], start=True, stop=True),
             reads=lb + [onesfb], writes=[pcrb])
        e3, e3b = e3r.next()
        S.op("act", lambda e, e3=e3, pcr=pcr: e.activation(out=e3[:], in_=pcr[:, 0:128], func=AF.Exp, scale=-1.0 / 16),
             reads=[pcrb], writes=[e3b])
        dec, decb = decr.next()
        S.op("act", lambda e, dec=dec, pcr=pcr: e.activation(out=dec[:, 0:1], in_=pcr[:, 128:129], func=AF.Exp, scale=-1.0 / 16),
             reads=[pcrb], writes=[decb])
        S.op("dve", lambda e, pcr=pcr: e.tensor_tensor(out=Dsum[:], in0=Dsum[:], in1=pcr[:, 128:129], op=ALU.add),
             reads=[pcrb, Dsumb], writes=[Dsumb])
        ktT, ktTb = ktTr.next()
        S.op("dve", lambda e, ktT=ktT, e3=e3, t=t: e.tensor_tensor(out=ktT[:], in0=gkT_fn(t), in1=e3[:], op=ALU.mult),
             reads=gkb + [e3b], writes=[ktTb])
        ptr, ptrb = c.tbanks.next()
        S.op("pe", lambda e, ptr=ptr, ktT=ktT: e.transpose(out=ptr[:, 0:128], in_=ktT[:], identity=c.ident[:]),
             reads=[ktTb, c.ident_b], writes=[ptrb])
        kt, ktb = ktr.next()
        S.op("act", lambda e, kt=kt, ptr=ptr: e.activation(out=kt[:], in_=ptr[:, 0:128], func=AF.Copy),
             reads=[ptrb], writes=[ktb])
        pu, pub = c.banks.next()
        S.op("pe", lambda e, pu=pu, kt=kt, t=t: e.matmul(pu[:, 0:256], lhsT=kt[:], rhs=gv_fn(t), start=True, stop=True),
             reads=[ktb] + gvb, writes=[pub])
        S.op("dve", lambda e, pu=pu, dec=dec: e.scalar_tensor_tensor(out=Sst[:], in0=Sst[:], scalar=dec[:, 0:1], in1=pu[:, 0:256],
                                                                      op0=ALU.mult, op1=ALU.add),
             reads=[pub, decb, Sstb], writes=[Sstb])


def attn_blocks(g):
    out = []
    for kb in range(20):
        D0 = 2048 - 128 * kb
        qs = [qt for qt in range(4) if (D0 + 128 * qt + 127 >= 0) and (D0 + 128 * qt - 127 <= 2048)]
        out.append((kb, D0, qs[0], qs[-1] + 1))
    return out


def build_B():
    c = Ctx("B")
    nc, S = c.nc, c.S
    c.din("hT", [D, TPC])
    d_qT = c.din("qT", [4, 128, TPC], BF16)
    d_kT = c.din("kT", [4, 128, TPC], BF16)
    d_vA = c.din("vA", [4, 128, NT, 130], BF16)
    d_hK = c.din("haloK", [4, 128, TPC], BF16)
    d_hV = c.din("haloV", [4, 128, NT, 130], BF16)
    d_gqT = c.din("gqT", [2, 128, TPC], BF16)
    d_gkT = c.din("gkT", [2, 128, TPC], BF16)
    d_gv = c.din("gv", [2, 128, NT, 256], BF16)
    d_gr = c.din("gr", [2, 128, NT, 256], BF16)
    d_lq = c.din("lq", [2, 128, NT, 128], F32)
    d_Sall = c.din("Sall", [128, NCORES, 2, 256], F32)
    d_Dall = c.din("Dall", [128, NCORES, 2], F32)
    d_pm = c.din("pmask", [128, NCORES], F32)
    w_out = c.din("w_out", [D, D])
    w_gate = c.din("w_gate", [D, DFF])
    w_up = c.din("w_up", [D, DFF])
    w_down = c.din("w_down", [DFF, D])
    c.din("g_attn", [128, 4])
    c.din("g_gla", [1, 512])
    c.din("g_post", [128, KC])
    c.din("g_fpre", [128, KC])
    c.din("g_fpost", [128, KC])
    c.din("ident", [128, 128])
    c.din("triincl", [128, 128])
    c.din("trirev", [128, 128])
    c.din("mask", [128, MASKW])
    o_h = c.dout("hT_out", [D, TPC])
    if DBG.get("B_dump"):
        o_mix = c.dout("mix_dump", [KC, 128, TPC], BF16)

    with ExitStack() as st:
        setup_common(c, st, nslab=0)
        hT, hTb = load_hT(c, st, "hT")
        gains = {}
        for nm, w in (("g_attn", 4), ("g_post", KC), ("g_fpre", KC), ("g_fpost", KC)):
            t, b = c.sb(st, [128, w], F32, nm)
            S.dma("sp", t[:], c.dr[nm], writes=[b])
            gains[nm] = (t, b)
        with ExitStack() as stm:
            mixT, mixTb = c.sb(stm, [128, KC, TPC], BF16, "mixT")
            with ExitStack() as st2:
                mask, maskb = c.sb(st2, [128, MASKW], BF16, "mask")
                S.dma("pool", mask[:], c.dr["mask"], writes=[maskb])
                QT, QTb = c.sb(st2, [128, TPC], BF16, "QT")
                KT, KTb = c.sb(st2, [128, 2 * TPC], BF16, "KT")
                VA, VAb = c.sb(st2, [128, 2 * NT, 130], BF16, "VA")
                Pr = Rot([c.sb(st2, [128, 512], BF16, "P") for _ in range(3)])
                ast_r = Rot([c.sb(st2, [128, 4, 128], BF16, "ast") for _ in range(2)])
                rden_r = Rot([c.sb(st2, [128, 4], F32, "rden") for _ in range(2)])
                Sbanks = Rot([c.banks.items[0], c.banks.items[1]])
                Obanks = Rot([c.banks.items[2], c.banks.items[3]])
                for p in range(4):
                    S.dma("sp", QT[:], d_qT[p], writes=[QTb])
                    S.dma("sp", KT[:, 0:TPC], d_hK[p], writes=[KTb])
                    S.dma("sp", KT[:, TPC:2 * TPC], d_kT[p], writes=[KTb])
                    S.dma("sp", VA[:, 0:NT, :], d_hV[p], writes=[VAb])
                    S.dma("sp", VA[:, NT:2 * NT, :], d_vA[p], writes=[VAb])
                    for g in range(NG):
                        ast, astb = ast_r.next()
                        for hh in range(2):
                            hs = slice(64 * hh, 64 * hh + 64)
                            Ops, Opsb = Obanks.next()
                            Ov = Ops[:, 0:260].rearrange("p (q x) -> p q x", x=65)
                            blocks = attn_blocks(g)
                            first = {}
                            last = {}
                            for (kb, D0, qlo, qhi) in blocks:
                                for qt in range(qlo, qhi):
                                    first.setdefault(qt, kb)
                                    last[qt] = kb
                            for (kb, D0, qlo, qhi) in blocks:
                                kcol = g * 512 + 128 * kb
                                qsl = slice(qlo * 128, qhi * 128)
                                Sps, Spsb = Sbanks.next()
                                S.op("pe", lambda e, Sps=Sps, kcol=kcol, qsl=qsl, qlo=qlo, qhi=qhi, g=g, hs=hs: e.matmul(
                                    Sps[:, qsl], lhsT=KT[hs, kcol:kcol + 128],
                                    rhs=QT[hs, g * 512 + qlo * 128:g * 512 + qhi * 128], start=True, stop=True),
                                     reads=[KTb, QTb], writes=[Spsb])
                                P, Pb = Pr.next()
                                S.op("act", lambda e, P=P, Sps=Sps, qsl=qsl: e.activation(out=P[:, qsl], in_=Sps[:, qsl],
                                                                                        func=AF.Exp, scale=0.125),
                                     reads=[Spsb], writes=[Pb])
                                mo = D0 + 384 + qlo * 128
                                S.op("dve", lambda e, P=P, qsl=qsl, mo=mo, qlo=qlo, qhi=qhi: e.tensor_tensor(
                                    out=P[:, qsl], in0=P[:, qsl], in1=mask[:, mo:mo + (qhi - qlo) * 128], op=ALU.mult),
                                     reads=[Pb, maskb], writes=[Pb])
                                vt = 4 * g + kb
                                for qt in range(qlo, qhi):
                                    S.op("pe", lambda e, P=P, qt=qt, vt=vt, hh=hh, Ov=Ov, kb=kb: e.matmul(
                                        Ov[:, qt, :], lhsT=P[:, qt * 128:(qt + 1) * 128], rhs=VA[:, vt, hh * 65:hh * 65 + 65],
                                        start=(kb == 0), stop=(last[qt] == kb), skip_group_check=True),
                                         reads=[Pb, VAb], writes=[Opsb], inc=(qt == qhi - 1))
                            rden, rdenb = rden_r.next()
                            S.op("dve", lambda e, rden=rden, Ov=Ov: e.reciprocal(out=rden[:], in_=Ov[:, :, 64]),
                                 reads=[Opsb], writes=[rdenb])
                            S.op("dve", lambda e, rden=rden, Ov=Ov, ast=ast, hs=hs: e.tensor_tensor(
                                out=ast[:, :, hs], in0=Ov[:, :, 0:64], in1=rden[:].unsqueeze(2).to_broadcast([128, 4, 64]),
                                op=ALU.mult), reads=[Opsb, rdenb], writes=[astb])
                        ptr, ptrb = c.tbanks.next()
                        for qt in range(4):
                            S.op("pe", lambda e, ptr=ptr, ast=ast, qt=qt: e.transpose(
                                out=ptr[:, qt * 128:(qt + 1) * 128], in_=ast[:, qt, :], identity=c.ident[:]),
                                 reads=[astb, c.ident_b], writes=[ptrb], inc=(qt == 3))
                        S.op("act", lambda e, ptr=ptr, p=p, g=g: e.activation(out=mixT[:, p, g * 512:(g + 1) * 512],
                                                                              in_=ptr[:, 0:512], func=AF.Copy),
                             reads=[ptrb], writes=[mixTb])
                S.barrier()
            ga, gab = gains["g_attn"]
            for g in range(NG):
                gs = slice(g * 512, (g + 1) * 512)
                rmsnorm_fm(c, lambda ci, gs=gs: mixT[:, ci, gs], [mixTb], 4, 512, ga, gab, g,
                           lambda ci, gs=gs: mixT[:, ci, gs], [mixTb])
            if DBG.get("B_stop") == 1:
                for k in range(KC):
                    S.dma("sp", o_mix[k], mixT[:, k, :], reads=[mixTb])
                S.finish("sp")
                return c
            with ExitStack() as st2:
                gla_full(c, st2, mixT, mixTb, d_gqT, d_gkT, d_gv, d_gr, d_lq, d_Sall, d_Dall, d_pm)
                S.barrier()
            if DBG.get("B_dump"):
                for k in range(KC):
                    S.dma("sp", o_mix[k], mixT[:, k, :], reads=[mixTb])
            if DBG.get("B_stop") in (2, 21, 22, 23, 24, 222, 224):
                S.finish("sp")
                return c
            with ExitStack() as st2:
                wo = [c.sb(st2, [128, KC, 512], BF16, "wo") for _ in range(2)]
                for i in range(2):
                    S.dma("pool", wo[i][0][:], w_out[:, i * 512:(i + 1) * 512].rearrange("(k p) n -> p k n", p=128),
                          writes=[wo[i][1]])
                mT, mTb = c.sb(st2, [128, KC, 512], F32, "mT")
                gpo, gpob = gains["g_post"]
                for g in range(NG):
                    gs = slice(g * 512, (g + 1) * 512)
                    for dc in range(KC):
                        pst, psb = proj_fm(c, wo[dc // 4][0], wo[dc // 4][1], dc % 4, mixT, mixTb, g)
                        S.op("act", lambda e, pst=pst, dc=dc: e.activation(out=mT[:, dc, :], in_=pst[:], func=AF.Copy),
                             reads=[psb], writes=[mTb])
                    rmsnorm_fm(c, lambda ci: mT[:, ci, :], [mTb], KC, D, gpo, gpob, g, None, [hTb],
                               add_into=lambda ci, gs=gs: hT[:, ci, gs])
                S.barrier()
        if DBG.get("B_stop") == 3:
            dst = o_h.rearrange("(k p) t -> p k t", p=128)
            for k in range(KC):
                S.dma("sp", dst[:, k, :], hT[:, k, :], reads=[hTb])
            S.finish("sp")
            return c
        with ExitStack() as st2:
            hn, hnb = c.sb(st2, [128, KC, TPC], BF16, "hn")
            gfp, gfpb = gains["g_fpre"]
            gfo, gfob = gains["g_fpost"]
            for g in range(NG):
                gs = slice(g * 512, (g + 1) * 512)
                rmsnorm_fm(c, lambda ci, gs=gs: hT[:, ci, gs], [hTb], KC, D, gfp, gfpb, g,
                           lambda ci, gs=gs: hn[:, ci, gs], [hnb])
            actT, actTb = c.sb(st2, [128, FC, 1024], BF16, "actT")
            fT, fTb = c.sb(st2, [128, KC, 512], F32, "fT")
            wd_r = Rot([c.sb(st2, [128, FC, 128], BF16, "wd") for _ in range(2)])
            sg_r = Rot([c.sb(st2, [128, 512], F32, "sg") for _ in range(2)])
            fslabs = Rot([c.sb(st2, [128, KC, 256], BF16, "fslab") for _ in range(4)])

            def load_fslab(wdram, col0):
                t, b = fslabs.next()
                S.dma("pool", t[:], wdram[:, col0:col0 + 256].rearrange("(k p) n -> p k n", p=128), writes=[b])
                return t, b

            for half in range(2):
                for fb in range(0, FC, 2):
                    sG, sGb = load_fslab(w_gate, fb * 128)
                    sU, sUb = load_fslab(w_up, fb * 128)
                    for j in range(2):
                        for g2 in range(2):
                            g = half * 2 + g2
                            pg, pgb = proj_fm(c, sG, sGb, j, hn, hnb, g)
                            pu, pub = proj_fm(c, sU, sUb, j, hn, hnb, g)
                            sgt, sgtb = sg_r.next()
                            S.op("act", lambda e, sgt=sgt, pg=pg: e.activation(out=sgt[:], in_=pg[:], func=AF.Silu),
                                 reads=[pgb], writes=[sgtb])
                            S.op("dve", lambda e, sgt=sgt, pu=pu, f=fb + j, g2=g2: e.tensor_tensor(
                                out=actT[:, f, g2 * 512:(g2 + 1) * 512], in0=sgt[:], in1=pu[:], op=ALU.mult),
                                 reads=[sgtb, pub], writes=[actTb])
                for g2 in range(2):
                    g = half * 2 + g2
                    gs = slice(g * 512, (g + 1) * 512)
                    for dc in range(KC):
                        wd, wdb = wd_r.next()
                        S.dma("pool", wd[:], w_down[:, dc * 128:(dc + 1) * 128].rearrange("(f p) n -> p f n", p=128), writes=[wdb])
                        pst, psb = c.banks.next()
                        for f in range(FC):
                            S.op("pe", lambda e, pst=pst, wd=wd, f=f, g2=g2: e.matmul(pst[:], lhsT=wd[:, f, :],
                                                                                     rhs=actT[:, f, g2 * 512:(g2 + 1) * 512],
                                                                                     start=(f == 0), stop=(f == FC - 1)),
                                 reads=[wdb, actTb], writes=[psb], inc=(f == FC - 1))
                        S.op("act", lambda e, pst=pst, dc=dc: e.activation(out=fT[:, dc, :], in_=pst[:], func=AF.Copy),
                             reads=[psb], writes=[fTb])
                    rmsnorm_fm(c, lambda ci: fT[:, ci, :], [fTb], KC, D, gfo, gfob, g, None, [hTb],
                               add_into=lambda ci, gs=gs: hT[:, ci, gs])
            S.barrier()
        dst = o_h.rearrange("(k p) t -> p k t", p=128)
        for k in range(KC):
            S.dma("sp", dst[:, k, :], hT[:, k, :], reads=[hTb])
        S.finish("sp")
    return c


def gla_full(c, st, mixT, mixTb, d_gqT, d_gkT, d_gv, d_gr, d_lq, d_Sall, d_Dall, d_pm):
    S = c.S
    triincl, triinclb = c.sb(st, [128, 128], F32, "triincl")
    trirev, trirevb = c.sb(st, [128, 128], F32, "trirev")
    causal, causalb = c.sb(st, [128, 128], BF16, "causal")
    S.dma("sp", triincl[:], c.dr["triincl"], writes=[triinclb])
    S.dma("sp", trirev[:], c.dr["trirev"], writes=[trirevb])
    S.dma("pool", causal[:], c.dr["triincl"], writes=[causalb])
    ggl, gglb = c.sb(st, [128, 512], F32, "ggl")
    S.dma("sp", ggl[:], c.dr["g_gla"].partition_broadcast(128), writes=[gglb])
    Sall, Sallb = c.sb(st, [128, NCORES, 2, 256], F32, "Sall")
    Dall, Dallb = c.sb(st, [128, NCORES, 2], F32, "Dall")
    pm, pmb = c.sb(st, [128, NCORES], F32, "pm")
    S.dma("sp", Sall[:], d_Sall, writes=[Sallb])
    S.dma("sp", Dall[:], d_Dall, writes=[Dallb])
    S.dma("sp", pm[:], d_pm, writes=[pmb])
    Am, Amb = c.sb(st, [128, NCORES, 2], F32, "Am")
    S.op("act", lambda e: e.activation(out=Am[:], in_=Dall[:], func=AF.Exp, scale=-1.0 / 16), reads=[Dallb], writes=[Amb])
    S.op("dve", lambda e: e.tensor_scalar(out=Am[:], in0=Am[:], scalar1=-1.0, scalar2=1.0, op0=ALU.add, op1=ALU.mult), reads=[Amb], writes=[Amb])
    S.op("dve", lambda e: e.tensor_tensor(out=Am[:], in0=Am[:], in1=pm[:].unsqueeze(2).to_broadcast([128, NCORES, 2]), op=ALU.mult),
         reads=[Amb, pmb], writes=[Amb])
    S.op("dve", lambda e: e.tensor_scalar(out=Am[:], in0=Am[:], scalar1=1.0, scalar2=1.0, op0=ALU.add, op1=ALU.mult), reads=[Amb], writes=[Amb])
    Sst = [c.sb(st, [128, 256], F32, "Sst") for _ in range(2)]
    Sbf = [c.sb(st, [128, 256], BF16, "Sbf") for _ in range(2)]
    tmpS, tmpSb = c.sb(st, [128, 256], F32, "tmpS")
    for gp in range(2):
        Sg, Sgb = Sst[gp]
        S.op("dve", lambda e, Sg=Sg: e.memset(Sg[:], 0.0), writes=[Sgb])
        for r in range(NCORES):
            S.op("dve", lambda e, r=r, gp=gp: e.tensor_scalar(out=tmpS[:], in0=Sall[:, r, gp, :], scalar1=pm[:, r:r + 1], scalar2=1.0,
                                                              op0=ALU.mult, op1=ALU.mult), reads=[Sallb, pmb], writes=[tmpSb])
            S.op("dve", lambda e, r=r, gp=gp, Sg=Sg: e.scalar_tensor_tensor(out=Sg[:], in0=Sg[:], scalar=Am[:, r, gp:gp + 1], in1=tmpS[:],
                                                                            op0=ALU.mult, op1=ALU.add),
                 reads=[Sgb, Amb, tmpSb], writes=[Sgb])
        S.op("act", lambda e, gp=gp, Sg=Sg: e.activation(out=Sbf[gp][0][:], in_=Sg[:], func=AF.Copy), reads=[Sgb], writes=[Sbf[gp][1]])

    gq, gqb = c.sb(st, [128, TPC], BF16, "gq")
    gk, gkb = c.sb(st, [128, TPC], BF16, "gk")
    gv, gvb = c.sb(st, [128, NT, 256], BF16, "gv")
    gr, grb = c.sb(st, [128, NT, 256], BF16, "gr")
    lq, lqb = c.sb(st, [128, NT, 128], F32, "lq")
    er = Rot([c.sb(st, [128, 3, 128], F32, "e123") for _ in range(2)])
    qkr = Rot([c.sb(st, [128, 3, 128], BF16, "qk3") for _ in range(2)])
    ktr = Rot([c.sb(st, [128, 128], BF16, "kt") for _ in range(2)])
    ATr = Rot([c.sb(st, [128, 2, 128], BF16, "AT") for _ in range(2)])
    decr = Rot([c.sb(st, [128, 1], F32, "dec") for _ in range(2)])
    ssr = Rot([c.sb(st, [128, 4], F32, "ss") for _ in range(2)])
    junk_r = Rot([c.sb(st, [128, 128], BF16, "junk") for _ in range(2)])
    silr = Rot([c.sb(st, [128, 256], F32, "sil") for _ in range(2)])
    t1r = Rot([c.sb(st, [128, 256], F32, "t1") for _ in range(2)])
    gor = Rot([c.sb(st, [128, 256], BF16, "go") for _ in range(2)])
    for gp in range(2):
        Sg, Sgb = Sst[gp]
        Sb, Sbb = Sbf[gp]
        S.dma("sp", gq[:], d_gqT[gp], writes=[gqb])
        S.dma("sp", gk[:], d_gkT[gp], writes=[gkb])
        S.dma("sp", gv[:], d_gv[gp], writes=[gvb])
        S.dma("sp", gr[:], d_gr[gp], writes=[grb])
        S.dma("sp", lq[:], d_lq[gp], writes=[lqb])
        if DBG.get("B_stop") == 21:
            return
        for t in range(NT):
            ts = slice(t * 128, (t + 1) * 128)
            if DBG.get("B_stop") == 24 and t == 1:
                return
            pc, pcb = c.banks.next()
            S.op("pe", lambda e, pc=pc, t=t: e.matmul(pc[:, 0:128], lhsT=lq[:, t, :], rhs=triincl[:], start=True, stop=True),
                 reads=[lqb, triinclb], writes=[pcb])
            S.op("pe", lambda e, pc=pc, t=t: e.matmul(pc[:, 128:256], lhsT=lq[:, t, :], rhs=trirev[:], start=True, stop=True),
                 reads=[lqb, trirevb], writes=[pcb])
            e3, e3b = er.next()
            S.op("act", lambda e, e3=e3, pc=pc: e.activation(out=e3[:, 0, :], in_=pc[:, 0:128], func=AF.Exp, scale=-1.0 / 16, bias=LN8),
                 reads=[pcb], writes=[e3b])
            S.op("act", lambda e, e3=e3, pc=pc: e.activation(out=e3[:, 1, :], in_=pc[:, 128:256], func=AF.Exp, scale=1.0 / 16, bias=LN8),
                 reads=[pcb], writes=[e3b])
            S.op("act", lambda e, e3=e3, pc=pc: e.activation(out=e3[:, 2, :], in_=pc[:, 128:256], func=AF.Exp, scale=-1.0 / 16),
                 reads=[pcb], writes=[e3b])
            dec, decb = decr.next()
            S.op("act", lambda e, dec=dec, pc=pc: e.activation(out=dec[:], in_=pc[:, 127:128], func=AF.Exp, scale=-1.0 / 16),
                 reads=[pcb], writes=[decb])
            qk, qkb = qkr.next()
            S.op("dve", lambda e, qk=qk, e3=e3, ts=ts: e.tensor_tensor(
                out=qk[:, 0:2, :], in0=e3[:, 0:2, :], in1=gq[:, ts].unsqueeze(1).to_broadcast([128, 2, 128]), op=ALU.mult),
                 reads=[e3b, gqb], writes=[qkb])
            S.op("dve", lambda e, qk=qk, e3=e3, ts=ts: e.tensor_tensor(out=qk[:, 2, :], in0=e3[:, 2, :], in1=gk[:, ts], op=ALU.mult),
                 reads=[e3b, gkb], writes=[qkb])
            if DBG.get("B_stop") == 222:
                return
            ptr, ptrb = c.tbanks.next()
            S.op("pe", lambda e, ptr=ptr, qk=qk: e.transpose(out=ptr[:, 0:128], in_=qk[:, 2, :], identity=c.ident[:]),
                 reads=[qkb, c.ident_b], writes=[ptrb])
            kt, ktb = ktr.next()
            S.op("act", lambda e, kt=kt, ptr=ptr: e.activation(out=kt[:], in_=ptr[:, 0:128], func=AF.Copy), reads=[ptrb], writes=[ktb])
            pa, pab = c.banks.next()
            for hh in range(2):
                hs = slice(64 * hh, 64 * hh + 64)
                S.op("pe", lambda e, pa=pa, qk=qk, hh=hh, hs=hs: e.matmul(pa[:, hh * 128:(hh + 1) * 128], lhsT=qk[hs, 2, :], rhs=qk[hs, 1, :],
                                                                         start=True, stop=True),
                     reads=[qkb], writes=[pab], inc=True, self_sync=(hh == 1))
            AT, ATb = ATr.next()
            S.op("dve", lambda e, AT=AT, pa=pa: e.tensor_tensor(out=AT[:], in0=pa[:, 0:256].rearrange("p (h x) -> p h x", h=2),
                                                              in1=causal[:].unsqueeze(1).to_broadcast([128, 2, 128]), op=ALU.mult),
                 reads=[pab, causalb], writes=[ATb])
            if DBG.get("B_stop") == 224:
                return
            po, pob = c.banks.next()
            for hh in range(2):
                hs = slice(64 * hh, 64 * hh + 64)
                S.op("pe", lambda e, po=po, AT=AT, hh=hh, t=t: e.matmul(po[:, hh * 128:(hh + 1) * 128], lhsT=AT[:, hh, :],
                                                                       rhs=gv[:, t, hh * 128:(hh + 1) * 128], start=True, stop=False),
                     reads=[ATb, gvb], writes=[pob], inc=False)
                S.op("pe", lambda e, po=po, qk=qk, hh=hh, hs=hs, Sb=Sb: e.matmul(po[:, hh * 128:(hh + 1) * 128], lhsT=qk[hs, 0, :],
                                                                              rhs=Sb[hs, hh * 128:(hh + 1) * 128], start=False, stop=True),
                     reads=[qkb, Sbb], writes=[pob], inc=True)
            if DBG.get("B_stop") == 22:
                return
            pu, pub = c.banks.next()
            S.op("pe", lambda e, pu=pu, kt=kt, t=t: e.matmul(pu[:, 0:256], lhsT=kt[:], rhs=gv[:, t, :], start=True, stop=True),
                 reads=[ktb, gvb], writes=[pub])
            S.op("dve", lambda e, pu=pu, dec=dec, Sg=Sg: e.scalar_tensor_tensor(out=Sg[:], in0=Sg[:], scalar=dec[:, 0:1], in1=pu[:, 0:256],
                                                                               op0=ALU.mult, op1=ALU.add),
                 reads=[pub, decb, Sgb], writes=[Sgb])
            S.op("act", lambda e, Sb=Sb, Sg=Sg: e.activation(out=Sb[:], in_=Sg[:], func=AF.Copy), reads=[Sgb], writes=[Sbb])
            if DBG.get("B_stop") == 23:
                return
            ss, ssb = ssr.next()
            S.op("dve", lambda e, ss=ss: e.memset(ss[:], 0.0), writes=[ssb])
            for hh in range(2):
                jk, jkb = junk_r.next()
                S.op("act", lambda e, jk=jk, po=po, hh=hh, ss=ss: e.activation(out=jk[:], in_=po[:, hh * 128:(hh + 1) * 128], func=AF.Square,
                                                                              accum_out=ss[:, hh:hh + 1]),
                     reads=[pob, ssb], writes=[jkb, ssb])
            S.op("act", lambda e, ss=ss: e.activation(out=ss[:, 2:4], in_=ss[:, 0:2], func=AF.Ln, scale=1.0 / 128, bias=EPS),
                 reads=[ssb], writes=[ssb])
            S.op("act", lambda e, ss=ss: e.activation(out=ss[:, 2:4], in_=ss[:, 2:4], func=AF.Exp, scale=-0.5), reads=[ssb], writes=[ssb])
            sil, silb = silr.next()
            S.op("act", lambda e, sil=sil, t=t: e.activation(out=sil[:], in_=gr[:, t, :], func=AF.Silu), reads=[grb], writes=[silb])
            t1, t1b = t1r.next()
            S.op("dve", lambda e, t1=t1, po=po, ss=ss: e.tensor_tensor(
                out=t1[:].rearrange("p (h x) -> p h x", h=2), in0=po[:, 0:256].rearrange("p (h x) -> p h x", h=2),
                in1=ss[:, 2:4].unsqueeze(2).to_broadcast([128, 2, 128]), op=ALU.mult), reads=[pob, ssb], writes=[t1b])
            S.op("dve", lambda e, t1=t1, gp=gp: e.tensor_tensor(out=t1[:], in0=t1[:], in1=ggl[:, gp * 256:(gp + 1) * 256], op=ALU.mult),
                 reads=[t1b, gglb], writes=[t1b])
            go, gob = gor.next()
            S.op("dve", lambda e, go=go, t1=t1, sil=sil: e.tensor_tensor(out=go[:], in0=t1[:], in1=sil[:], op=ALU.mult),
                 reads=[t1b, silb], writes=[gob])
            ptr2, ptr2b = c.tbanks.next()
            for hh in range(2):
                S.op("pe", lambda e, ptr2=ptr2, go=go, hh=hh: e.transpose(out=ptr2[:, hh * 128:(hh + 1) * 128],
                                                                         in_=go[:, hh * 128:(hh + 1) * 128], identity=c.ident[:]),
                     reads=[gob, c.ident_b], writes=[ptr2b], inc=(hh == 1))
            S.op("act", lambda e, ptr2=ptr2, gp=gp, ts=ts: e.activation(
                out=mixT[:, 4 + 2 * gp:6 + 2 * gp, ts], in_=ptr2[:, 0:256].rearrange("p (h x) -> p h x", h=2), func=AF.Copy),
                 reads=[ptr2b], writes=[mixTb])


_CACHE = {}


def _prog(name):
    if name not in _CACHE:
        _CACHE[name] = build_A() if name == "A" else build_B()
    return _CACHE[name]


def _m(d):
    d = int(d)
    if d < 0:
        return 0
    return int(d <= 128) + int(d <= 512 and d % 4 == 0) + int(d <= 2048 and d % 16 == 0)


def _consts():
    ident = np.eye(128, dtype=np.float32)
    s = np.arange(128)
    triincl = (s[:, None] <= s[None, :]).astype(np.float32)
    trirev = (s[:, None] > s[None, :]).astype(np.float32)
    rm = np.zeros((128, 128), np.float32)
    for m in range(128):
        ch = m % 64
        if ch < 8:
            rm[m + 8, m] = 1.0
        elif ch < 16:
            rm[m - 8, m] = 1.0
    y = np.arange(MASKW)[None, :]
    p = np.arange(128)[:, None]
    dd = y - p - 384
    mvals = np.vectorize(_m)(dd).astype(np.float32)
    return dict(ident=ident, triincl=triincl, trirev=trirev, rm=rm, mask=mvals)


def _rope_tables(core):
    half = 8
    inv = (np.float32(500000.0) ** (-(np.arange(half, dtype=np.float32) * np.float32(2.0)) / np.float32(16))).astype(np.float32)
    pos = (core * TPC + np.arange(TPC)).astype(np.float32)
    ang = pos[:, None] * inv[None, :]
    cos = np.cos(ang).astype(np.float32).T
    sin = np.sin(ang).astype(np.float32).T
    C = np.ones((128, TPC), np.float32)
    Sg = np.zeros((128, TPC), np.float32)
    for m in range(128):
        ch = m % 64
        if ch < 8:
            C[m] = cos[ch]
            Sg[m] = -sin[ch]
        elif ch < 16:
            C[m] = cos[ch - 8]
            Sg[m] = sin[ch - 8]
    return C, Sg


def _gl(v):
    return np.ascontiguousarray(v.reshape(-1, 128).T.astype(np.float32))


def _run(prog, in_maps):
    res = run_bass_kernel_spmd(prog.nc, in_maps, core_ids=list(range(NCORES)))
    return res.results


def run_A(hT_list, P, l, cst, rope):
    ins = []
    for cidx in range(NCORES):
        ins.append(dict(hT=hT_list[cidx], w_in=P["w_in"][l], wgu=P["gla_w_gate_up"][l], bg=P["gla_b_gate"][l][None, :],
                        g_pre=_gl(P["mix_pre_norm"][l]), ident=cst["ident"], trirev=cst["trirev"], rm=cst["rm"],
                        ropeC=rope[cidx][0], ropeS=rope[cidx][1]))
    return _run(_prog("A"), ins)


def run_B(hT_list, A_out, P, l, cst):
    Sall = np.ascontiguousarray(np.stack([A_out[r]["Sloc"] for r in range(NCORES)], axis=1))
    Dall = np.ascontiguousarray(np.stack([A_out[r]["Dtot"][:, :, 0].T for r in range(NCORES)], axis=1))
    ins = []
    for cidx in range(NCORES):
        a = A_out[cidx]
        if cidx == 0:
            hK = np.zeros_like(a["kT"])
            hV = np.zeros_like(a["vA"])
        else:
            hK = A_out[cidx - 1]["kT"]
            hV = A_out[cidx - 1]["vA"]
        pm = np.zeros((128, NCORES), np.float32)
        pm[:, :cidx] = 1.0
        ins.append(dict(hT=hT_list[cidx], qT=a["qT"], kT=a["kT"], vA=a["vA"], haloK=hK, haloV=hV, gqT=a["gqT"], gkT=a["gkT"],
                        gv=a["gv"], gr=a["gr"], lq=a["lq"], Sall=Sall, Dall=Dall, pmask=pm,
                        w_out=P["w_out"][l], w_gate=P["w_gate"][l], w_up=P["w_up"][l], w_down=P["w_down"][l],
                        g_attn=_gl(P["attn_out_norm"][l]), g_gla=P["gla_out_norm"][l][None, :].astype(np.float32),
                        g_post=_gl(P["mix_post_norm"][l]), g_fpre=_gl(P["ffn_pre_norm"][l]), g_fpost=_gl(P["ffn_post_norm"][l]),
                        ident=cst["ident"], triincl=cst["triincl"], trirev=cst["trirev"], mask=cst["mask"]))
    return _run(_prog("B"), ins)


def kernel(**inputs):
    P = {k: np.asarray(v, dtype=np.float32) for k, v in inputs.items()}
    x = P["x"][0]
    cst = _consts()
    rope = [_rope_tables(cidx) for cidx in range(NCORES)]
    hT = [np.ascontiguousarray(x[cidx * TPC:(cidx + 1) * TPC].T) for cidx in range(NCORES)]
    for l in range(2):
        A_out = run_A(hT, P, l, cst, rope)
        B_out = run_B(hT, A_out, P, l, cst)
        hT = [B_out[cidx]["hT_out"] for cidx in range(NCORES)]
    out = np.concatenate([h.T for h in hT], axis=0)[None]
    return np.ascontiguousarray(out.astype(np.float32))
```
